# Optimizing a Trainium2 kernel written in Bass

```python
import jax, jax.numpy as jnp
from jax import lax
import numpy as np


D_MODEL = 1024
BATCH = 4
SEQ = 8192
DEPTH = 1

CHUNK = 64
GM_BLOCK = 128
GM_GROUP_DIM = 128
GM_WIDTH = D_MODEL
GM_GROUPS = GM_WIDTH // GM_GROUP_DIM
HG_DK = 128
HG_DV = 128
HG_HEADS = D_MODEL // 128
HG_KWIDTH = HG_HEADS * HG_DK
HG_WIDTH = HG_HEADS * HG_DV
HG_CHUNK = 16
D_FF = -((-8 * D_MODEL) // (3 * 256)) * 256
PLE_DIM = 256
ALPHA = (2.0 * DEPTH) ** 0.25
BETA = (8.0 * DEPTH) ** -0.25
LN_EPS = 1e-5
RMS_EPS = 1e-6
IN_SPLITS = (GM_WIDTH, GM_WIDTH, HG_KWIDTH, HG_KWIDTH, HG_WIDTH, HG_WIDTH, D_MODEL, D_MODEL)
D_IN = sum(IN_SPLITS)
IN_OFFSETS = tuple(int(o) for o in np.cumsum(IN_SPLITS)[:-1])

kernel_name = "hybrid_gmlp_hgrn2_deepnorm_block"


def _layer_norm(x, g, b):
    xf = x.astype(jnp.float32)
    xc = xf - jnp.mean(xf, -1, keepdims=True)
    var = jnp.mean(xc * xc, -1, keepdims=True)
    y = xc * lax.rsqrt(var + LN_EPS) * g.astype(jnp.float32) + b.astype(jnp.float32)
    return y.astype(x.dtype)


def _rms_norm(x, g):
    return x * lax.rsqrt(jnp.mean(x * x, -1, keepdims=True) + RMS_EPS) * g


def _gmlp_spatial_gate(u, v, v_g, v_b, w_s, b_s):
    bsz, seq, _ = v.shape
    pos = jnp.arange(GM_BLOCK)
    mask = (pos[:, None] // CHUNK) >= (pos[None, :] // CHUNK)
    w = jnp.where(mask[None], w_s, jnp.zeros_like(w_s))
    vn = _layer_norm(v, v_g, v_b).reshape(bsz, seq // GM_BLOCK, GM_BLOCK, GM_GROUPS, GM_GROUP_DIM)
    mixed = jnp.einsum("gts,bnsgc->bntgc", w, vn) + b_s.T[:, :, None]
    return u * mixed.reshape(bsz, seq, GM_WIDTH)


def _hgrn2(q_raw, f_raw, i_raw, g_raw, lb, o_g):
    bsz, seq, _ = q_raw.shape
    nc = seq // HG_CHUNK
    f32 = jnp.float32

    def heads(t, d):
        return t.astype(f32).reshape(bsz, nc, HG_CHUNK, HG_HEADS, d)

    lbf = lb.astype(f32)
    fr = f_raw.astype(f32)
    log_f = heads(jnp.log(lbf + (1.0 - lbf) * jax.nn.sigmoid(fr)), HG_DK)
    k = heads((1.0 - lbf) * jax.nn.sigmoid(-fr), HG_DK)
    q = heads(jax.nn.silu(q_raw.astype(f32)), HG_DK)
    v = heads(i_raw, HG_DV)
    b = jnp.cumsum(log_f, axis=2)
    q_dec = q * jnp.exp(b)
    k_dec = k * jnp.exp(-b)
    causal = jnp.tril(jnp.ones((HG_CHUNK, HG_CHUNK), f32))
    scores = jnp.einsum("bnthd,bnshd->bnhts", q_dec, k_dec) * causal
    o_intra = jnp.einsum("bnhts,bnshe->bnthe", scores, v)
    b_last = b[:, :, -1:]
    k_end = k * jnp.exp(b_last - b)
    decay = jnp.exp(b_last[:, :, 0])

    def step(state, xs):
        qc, kc, vc, dc = xs
        o = jnp.einsum("bthd,bhde->bthe", qc, state)
        state = dc[..., None] * state + jnp.einsum("bthd,bthe->bhde", kc, vc)
        return state, o

    s0 = jnp.zeros((bsz, HG_HEADS, HG_DK, HG_DV), f32)
    _, o_inter = lax.scan(step, s0, (jnp.moveaxis(q_dec, 1, 0), jnp.moveaxis(k_end, 1, 0),
                                     jnp.moveaxis(v, 1, 0), jnp.moveaxis(decay, 1, 0)))
    o = (o_intra + jnp.moveaxis(o_inter, 0, 1)).reshape(bsz, seq, HG_HEADS, HG_DV)
    o = _rms_norm(o, o_g.astype(f32).reshape(HG_HEADS, HG_DV)).reshape(bsz, seq, HG_WIDTH)
    o = o * jax.nn.silu(g_raw.astype(f32))
    return o.astype(q_raw.dtype)


def setup_inputs(seed: int = 0) -> dict:
    key = jax.random.key(seed)
    ks = jax.random.split(key, 26)
    f32 = jnp.float32
    L = DEPTH

    def nrm(k, shape, scale):
        return jax.random.normal(k, shape, f32) * scale

    def gain(k, shape):
        return 1.0 + nrm(k, shape, 0.02)

    return {
        "x": nrm(ks[0], (BATCH, SEQ, D_MODEL), 1.0),
        "p": nrm(ks[1], (DEPTH, BATCH, SEQ, PLE_DIM), 1.0),
        "ln0_g": gain(ks[2], (D_MODEL,)),
        "ln0_b": nrm(ks[3], (D_MODEL,), 0.02),
        "w_in": nrm(ks[4], (L, D_MODEL, D_IN), D_MODEL ** -0.5),
        "b_in": nrm(ks[5], (L, D_IN), 0.02),
        "gm_norm_g": gain(ks[6], (L, GM_WIDTH)),
        "gm_norm_b": nrm(ks[7], (L, GM_WIDTH), 0.02),
        "gm_w_s": nrm(ks[8], (L, GM_GROUPS, GM_BLOCK, GM_BLOCK), 0.5 * GM_BLOCK ** -0.5),
        "gm_b_s": gain(ks[9], (L, GM_GROUPS, GM_BLOCK)),
        "hg_lb_logits": nrm(ks[10], (L + 1, HG_KWIDTH), 0.5),
        "hg_norm_g": gain(ks[11], (L, HG_WIDTH)),
        "w_a": nrm(ks[12], (L, GM_WIDTH, D_MODEL), BETA * GM_WIDTH ** -0.5),
        "w_b": nrm(ks[13], (L, HG_WIDTH, D_MODEL), BETA * HG_WIDTH ** -0.5),
        "w_o": nrm(ks[14], (L, D_MODEL, D_MODEL), BETA * D_MODEL ** -0.5),
        "ln1_g": gain(ks[15], (L, D_MODEL)),
        "ln1_b": nrm(ks[16], (L, D_MODEL), 0.02),
        "w_ffn_gate": nrm(ks[17], (L, D_MODEL, D_FF), D_MODEL ** -0.5),
        "w_ffn_up": nrm(ks[18], (L, D_MODEL, D_FF), BETA * D_MODEL ** -0.5),
        "w_ffn_down": nrm(ks[19], (L, D_FF, D_MODEL), BETA * D_FF ** -0.5),
        "w_ple": nrm(ks[20], (L, PLE_DIM, D_MODEL), PLE_DIM ** -0.5),
        "w_ple_gate": nrm(ks[21], (L, D_MODEL, D_MODEL), D_MODEL ** -0.5),
        "b_ple_gate": nrm(ks[22], (L, D_MODEL), 0.02),
        "ln2_g": gain(ks[23], (L, D_MODEL)),
        "ln2_b": nrm(ks[24], (L, D_MODEL), 0.02),
    }


def reference(x, p, ln0_g, ln0_b, w_in, b_in, gm_norm_g, gm_norm_b, gm_w_s, gm_b_s,
              hg_lb_logits, hg_norm_g, w_a, w_b, w_o, ln1_g, ln1_b,
              w_ffn_gate, w_ffn_up, w_ffn_down, w_ple, w_ple_gate, b_ple_gate, ln2_g, ln2_b):
    lb_all = jnp.cumsum(jax.nn.softmax(hg_lb_logits.astype(jnp.float32), axis=0), axis=0)
    h = _layer_norm(x, ln0_g, ln0_b)
    for i in range(DEPTH):
        proj = jnp.einsum("bsd,de->bse", h, w_in[i]) + b_in[i]
        u_r, v_r, q_r, f_r, i_r, g_r, ga_r, gb_r = jnp.split(proj, IN_OFFSETS, axis=-1)
        y_a = _gmlp_spatial_gate(jax.nn.gelu(u_r), jax.nn.gelu(v_r), gm_norm_g[i], gm_norm_b[i],
                                 gm_w_s[i], gm_b_s[i])
        y_b = _hgrn2(q_r, f_r, i_r, g_r, lb_all[i], hg_norm_g[i])
        merged = (jax.nn.sigmoid(ga_r) * jnp.einsum("bsc,cd->bsd", y_a, w_a[i])
                  + jax.nn.sigmoid(gb_r) * jnp.einsum("bsc,cd->bsd", y_b, w_b[i]))
        mix = jnp.einsum("bsd,de->bse", merged, w_o[i])
        h1 = _layer_norm(ALPHA * h + mix, ln1_g[i], ln1_b[i])
        hid = jax.nn.silu(jnp.einsum("bsd,df->bsf", h1, w_ffn_gate[i])) * jnp.einsum("bsd,df->bsf", h1, w_ffn_up[i])
        ffn = jnp.einsum("bsf,fd->bsd", hid, w_ffn_down[i])
        ple = (jax.nn.sigmoid(jnp.einsum("bsd,de->bse", h1, w_ple_gate[i]) + b_ple_gate[i])
               * jnp.einsum("bsr,rd->bsd", p[i], w_ple[i]))
        h = _layer_norm(ALPHA * h1 + ffn + ple, ln2_g[i], ln2_b[i])
    return h
```

```python
import numpy as np
import concourse.bass as bass
import concourse.mybir as mybir
from concourse.bass_utils import run_bass_kernel_spmd

F32 = mybir.dt.float32
BF16 = mybir.dt.bfloat16
AF = mybir.ActivationFunctionType
ALU = mybir.AluOpType

D = 1024
T = 512
NB = 4
NH = 8
DFF = 2816
NFC = 22
PLE = 256
ALPHA = 2.0 ** 0.25
LN_EPS = 1e-5
RMS_EPS = 1e-6
NTOK = 4096
NSLOT = 5
SLOT_ELEMS = 4096
WD_SPLIT = [6, 6, 5, 5]

WSPEC = {
    "win": (16, 8, 512), "wa": (2, 8, 512), "wb": (2, 8, 512), "wo": (2, 8, 512),
    "wg": (11, 8, 256), "wu": (11, 8, 256), "wd": (8, 6, 512), "wpg": (2, 8, 512),
    "wple": (1, 2, 1024),
}
COLS = {}
_c = 0
for _n, _w in [("g0", 8), ("b0", 8), ("bin", 64), ("gmg", 8), ("gmb", 8), ("l0", 8), ("l1", 8),
               ("og", 8), ("g1", 8), ("b1", 8), ("flag", 1), ("lb", 8), ("oml", 8), ("noml", 8)]:
    COLS[_n] = (_c, _w)
    _c += _w
NCOL = _c
ROWS = ["G0", "B0", "G1", "B1", "G2", "B2"]


class Op:
    __slots__ = ("eng", "fn", "deps", "sem", "isdma", "ndma", "sig", "val", "seq")


class Prog:
    def __init__(self):
        self.ops = {e: [] for e in ("pe", "act", "dve", "pool", "sp")}
        self.lastw = {}
        self.rd = {}
        self.seq = 0
        self.semops = {}

    def add(self, eng, fn, r=(), w=(), dma=None, ndma=1):
        o = Op()
        o.eng = eng
        o.fn = fn
        o.isdma = dma is not None
        o.ndma = ndma
        o.sem = ("dma",) + tuple(dma) if dma is not None else eng
        o.sig = o.isdma
        o.val = 0
        o.seq = self.seq
        self.seq += 1
        deps = {}

        def adddep(d):
            if d is o:
                return
            if eng == "pe" and d.eng == "pe" and not d.isdma:
                return
            cur = deps.get(d.sem)
            if cur is None or cur.seq < d.seq:
                deps[d.sem] = d

        for k in r:
            d = self.lastw.get(k)
            if d is not None:
                adddep(d)
        for k in w:
            d = self.lastw.get(k)
            if d is not None:
                adddep(d)
            for x in self.rd.get(k, {}).values():
                adddep(x)
        o.deps = list(deps.values())
        for d in o.deps:
            d.sig = True
        for k in r:
            self.rd.setdefault(k, {})[o.sem] = o
        for k in w:
            self.lastw[k] = o
            self.rd[k] = {}
        self.ops[eng].append(o)
        self.semops.setdefault(o.sem, []).append(o)
        return o

    def finalize(self):
        for sem, ops in self.semops.items():
            tot = 0
            for o in ops:
                if o.isdma:
                    tot += 16 * o.ndma
                    o.val = tot
                elif o.sig:
                    tot += 1
                    o.val = tot

    def emit(self, eng, e, sems):
        waited = {}
        for o in self.ops[eng]:
            for d in o.deps:
                if waited.get(d.sem, 0) < d.val:
                    e.wait_ge(sems[d.sem], d.val)
                    waited[d.sem] = d.val
            if o.fn is None:
                continue
            ins = o.fn(e)
            if o.isdma:
                for i in ins:
                    i.then_inc(sems[o.sem], 16)
            elif o.sig:
                ins.then_inc(sems[o.sem], 1)


def build_program(NLT, NFT, stage=2):
    nc = bass.Bass("TRN2", target_bir_lowering=False)
    P = Prog()
    ntl = max(NLT, 1) * T
    ntf = NFT * T

    def din(name, shape, dt=F32):
        return nc.dram_tensor(name, list(shape), dt, kind="ExternalInput")

    xl = din("xl", [ntl, D])
    xf = din("xf", [ntf, D])
    pf = din("pf", [ntf, PLE])
    yo = nc.dram_tensor("y", [ntf, D], F32, kind="ExternalOutput")
    w32 = {}
    wbf = {}
    for name, (nb, kc, ncol) in WSPEC.items():
        w32[name] = din(name, [nb, 128, kc * ncol])
        wbf[name] = nc.dram_tensor(name + "_bf", [nb, 128, kc * ncol], BF16)
    cols_d = din("cols", [128, NCOL])
    rows_d = din("rows", [128, len(ROWS) * D])
    wst_d = din("wst", [128, 8 * 128])
    bsb_d = din("bsb", [128, 8 * 128])
    cmask_d = din("cmask", [128, 128])
    reset_d = din("reset", [128, T])
    ident_d = din("ident", [128, 128])
    brow_d = din("brow", [1, 2 * D])

    def sb(name, shape, dt):
        return nc.alloc_sbuf_tensor(name, list(shape), dt)

    wslot = [sb(f"wslot{i}", [128, SLOT_ELEMS], BF16) for i in range(NSLOT)]
    xin = [sb(f"xin{i}", [128, D], F32) for i in range(2)]
    res = sb("res", [128, NB, D], F32)
    hT = sb("hT", [128, 8, T], BF16)
    Abuf = sb("Abuf", [128, 24, T], BF16)
    vn = sb("vn", [128, NB, D], BF16)
    vtm = sb("vtm", [128, NB, D], BF16)
    cols = sb("cols_sb", [128, NCOL], F32)
    rows = sb("rows_sb", [128, len(ROWS), D], F32)
    wst32 = sb("wst32", [128, 8, 128], F32)
    wstb = sb("wstb", [128, 8, 128], BF16)
    Cg = sb("Cg", [128, 8, 128], F32)
    cmask = sb("cmask_sb", [128, 128], F32)
    reset = sb("reset_sb", [128, T], F32)
    ident32 = sb("ident32", [128, 128], F32)
    identb = sb("identb", [128, 128], BF16)
    onesb = sb("onesb", [128, 128], BF16)
    brow32 = sb("brow32", [1, 2 * D], F32)
    browb = sb("browb", [1, 2 * D], BF16)
    S32 = sb("S32", [128, NH, 128], F32)
    Sbf = sb("Sbf", [128, NH, 128], BF16)
    t32 = [sb(f"t32_{i}", [128, T], F32) for i in range(6)]
    qd = [sb(f"qd{i}", [128, T], BF16) for i in range(2)]
    kd = [sb(f"kd{i}", [128, T], BF16) for i in range(2)]
    ke = [sb(f"ke{i}", [128, T], BF16) for i in range(2)]
    ketm = [sb(f"ketm{i}", [128, NB, 128], BF16) for i in range(2)]
    gsb = [sb(f"gs{i}", [128, T], BF16) for i in range(2)]
    dec = [sb(f"dec{i}", [128, 16], F32) for i in range(2)]
    osq = sb("osq", [128, T], BF16)
    scb = [sb(f"scb{i}", [128, 128], BF16) for i in range(4)]
    st = [sb(f"st{i}", [128, 12], F32) for i in range(2)]
    mv = [sb(f"mv{i}", [128, 2], F32) for i in range(2)]
    rs = [sb(f"rs{i}", [128, 2], F32) for i in range(2)]
    pin = [sb(f"pin{i}", [128, PLE], F32) for i in range(2)]
    pT = sb("pT", [128, 2, T], BF16)
    ps = [nc.alloc_psum_tensor(f"ps{i}", [128, 512], F32) for i in range(8)]

    K = [("t32", i) for i in range(6)]

    def pk(b):
        return [("ps", b)]

    def col(name, i=0, n=1):
        c0, _ = COLS[name]
        return cols[:, c0 + i:c0 + i + n]

    def mm(out, lhsT, rhs, start, stop, r, w, **kw):
        P.add("pe", lambda e: e.matmul(out, lhsT, rhs, start=start, stop=stop, **kw), r, w)

    def tr(out, in_, idn, r, w):
        P.add("pe", lambda e: e.transpose(out, in_, idn), r, w)

    def act(out, in_, func, r, w, bias=None, scale=None):
        kw = {}
        if bias is not None:
            kw["bias"] = bias
        if scale is not None:
            kw["scale"] = scale
        P.add("act", lambda e: e.activation(out=out, in_=in_, func=func, **kw), r, w)

    def tt(eng, out, a, b, op, r, w):
        P.add(eng, lambda e: e.tensor_tensor(out=out, in0=a, in1=b, op=op), r, w)

    def ts(eng, out, a, s1, s2, op0, op1, r, w):
        P.add(eng, lambda e: e.tensor_scalar(out=out, in0=a, scalar1=s1, scalar2=s2, op0=op0, op1=op1), r, w)

    def stt(eng, out, a, s, b, op0, op1, r, w):
        P.add(eng, lambda e: e.scalar_tensor_tensor(out=out, in0=a, scalar=s, in1=b, op0=op0, op1=op1), r, w)

    def cp(eng, out, in_, r, w):
        if eng == "act":
            P.add("act", lambda e: e.copy(out=out, in_=in_), r, w)
        else:
            P.add(eng, lambda e: e.tensor_copy(out=out, in_=in_), r, w)

    def dma(q, out, in_, r, w, key):
        P.add(q, lambda e: [e.dma_start(out=out, in_=in_)], r, w, dma=key)

    wk = [0]

    def workbank():
        i = wk[0] % 4
        wk[0] += 1
        return i

    class WStream:
        def __init__(self):
            self.seq = []
            self.issued = 0
            self.taken = 0
            self.free = [True] * NSLOT

        def pump(self):
            while self.issued < len(self.seq) and self.issued < self.taken + NSLOT:
                s = self.issued % NSLOT
                if not self.free[s]:
                    break
                name, b = self.seq[self.issued]
                nb, kc, ncol = WSPEC[name]
                self.free[s] = False
                dma("sp", wslot[s][:, 0:kc * ncol], wbf[name][b], [("wbf", name, b)], [("w", s)], ("w", s))
                self.issued += 1

        def take(self, name, b):
            assert self.seq[self.taken] == (name, b), (self.seq[self.taken], name, b)
            self.pump()
            assert self.issued > self.taken, "weight ring deadlock"
            s = self.taken % NSLOT
            self.taken += 1
            nb, kc, ncol = WSPEC[name]
            return s, wslot[s][:, 0:kc * ncol].rearrange("p (k n) -> p k n", k=kc)

        def release(self, s):
            self.free[s] = True
            self.pump()

    WS = WStream()

    def light_seq():
        return [("win", 8), ("win", 9), ("win", 6), ("win", 7)]

    def full_seq():
        s = [("win", 0), ("win", 1), ("win", 2), ("win", 3), ("win", 8), ("win", 9),
             ("win", 4), ("win", 6), ("win", 10), ("win", 5), ("win", 7), ("win", 11)]
        for sl in range(2):
            s += [("wa", sl), ("win", 12 + sl), ("wb", sl), ("win", 14 + sl)]
        s += [("wo", 0), ("wo", 1)]
        for i in range(11):
            s += [("wg", i), ("wu", i)]
        s += [("wd", i) for i in range(8)]
        s += [("wple", 0), ("wpg", 0), ("wpg", 1)]
        return s

    for _ in range(NLT):
        WS.seq += light_seq()
    if stage < 1:
        WS.seq = []
    if stage >= 2:
        for _ in range(NFT):
            WS.seq += full_seq()

    cast_groups = [
        [("win", 6), ("win", 7), ("win", 8), ("win", 9)],
        [("win", 0), ("win", 1), ("win", 2), ("win", 3)],
        [("win", 4), ("win", 5), ("win", 10), ("win", 11)],
        [("win", 12), ("win", 13), ("win", 14), ("win", 15), ("wa", 0), ("wa", 1), ("wb", 0), ("wb", 1)],
        [("wo", 0), ("wo", 1)] + [("wg", i) for i in range(11)],
        [("wu", i) for i in range(11)],
        [("wd", i) for i in range(8)] + [("wpg", 0), ("wpg", 1), ("wple", 0)],
    ]

    def emit_cast(gi):
        grp = cast_groups[gi]

        def fn(e, grp=grp):
            return [e.dma_start(out=wbf[n][b], in_=w32[n][b]) for (n, b) in grp]
        P.add("pool", fn, [], [("wbf", n, b) for (n, b) in grp], dma=("cast", gi), ndma=len(grp))

    def setup():
        cl = [(cols[:, :], cols_d[:, :]), (rows[:, :, :].rearrange("p a b -> p (a b)"), rows_d[:, :]),
              (wst32[:, :, :].rearrange("p a b -> p (a b)"), wst_d[:, :]),
              (Cg[:, :, :].rearrange("p a b -> p (a b)"), bsb_d[:, :]),
              (cmask[:, :], cmask_d[:, :]), (reset[:, :], reset_d[:, :]), (ident32[:, :], ident_d[:, :]),
              (brow32[:, :], brow_d[:, :])]

        def fn(e):
            return [e.dma_start(out=o, in_=i) for (o, i) in cl]
        P.add("sp", fn, [], ["const"], dma=("const",), ndma=len(cl))
        emit_cast(0)
        cp("dve", identb[:, :], ident32[:, :], ["const"], ["identb"])
        P.add("pool", lambda e: e.memset(onesb[:, :], 1.0), [], ["onesb"])
        cp("dve", browb[:, :], brow32[:, :], ["const"], ["browb"])
        tt("dve", col("lb", 0, 8), col("l0", 0, 8), col("l1", 0, 8), ALU.subtract, ["const"], ["lbtmp"])
        act(col("lb", 0, 8), col("lb", 0, 8), AF.Sigmoid, ["lbtmp"], ["lb"])
        ts("dve", col("oml", 0, 8), col("lb", 0, 8), -1.0, 1.0, ALU.mult, ALU.add, ["lb"], ["oml"])
        ts("dve", col("noml", 0, 8), col("oml", 0, 8), -1.0, 0.0, ALU.mult, ALU.add, ["oml"], ["noml"])
        P.add("pool", lambda e: e.memset(wst32[64:128, :, 0:64], 0.0), ["const"], ["wst32m"])
        cp("dve", wstb[:, :, :], wst32[:, :, :], ["wst32m"], ["wstb"])
        for g in range(8):
            bi = workbank()
            mm(ps[bi][:, 0:128], onesb[:, :], wstb[:, g, :], True, True, ["onesb", "wstb"], [("ps", bi)])
            stt("dve", Cg[:, g, :], ps[bi][:, 0:128], col("gmb", g), Cg[:, g, :], ALU.mult, ALU.add,
                [("ps", bi), "const"], [("Cg", g)])
        P.add("pool", lambda e: e.memset(S32[:, :, :], 0.0), [], [("S32", h) for h in range(NH)])
        P.add("pool", lambda e: e.memset(Sbf[:, :, :], 0.0), [], [("Sbf", h) for h in range(NH)])

    lnc = [0]

    def ln_stats(h0, h1, rkeys):
        i = lnc[0] % 2
        lnc[0] += 1
        P.add("dve", lambda e: e.bn_stats(out=st[i][:, 0:6], in_=h0), rkeys, [("st", i, 0)])
        P.add("dve", lambda e: e.bn_stats(out=st[i][:, 6:12], in_=h1), rkeys, [("st", i, 1)])
        P.add("dve", lambda e: e.bn_aggr(out=mv[i][:, :], in_=st[i][:, :]), [("st", i, 0), ("st", i, 1)], [("mv", i)])
        act(rs[i][:, 0:1], mv[i][:, 1:2], AF.Ln, [("mv", i)], [("rs", i, 0)], bias=LN_EPS)
        act(rs[i][:, 0:1], rs[i][:, 0:1], AF.Exp, [("rs", i, 0)], [("rs", i, 0)], scale=-0.5)
        stt("dve", rs[i][:, 1:2], mv[i][:, 0:1], -1.0, rs[i][:, 0:1], ALU.mult, ALU.mult,
            [("mv", i), ("rs", i, 0)], [("rs", i, 1)])
        return i

    def ln0_block(xsrc, t0, j, light):
        s = (ln0_block.cnt) % 2
        ln0_block.cnt += 1
        dma("sp", xin[s][:, :], xsrc[t0 + j * 128:t0 + (j + 1) * 128, :], [], [("xin", s)], ("x", s))
        i = ln_stats(xin[s][:, 0:512], xin[s][:, 512:1024], [("xin", s)])
        act(res[:, j, :], xin[s][:, :], AF.Identity, [("xin", s), ("rs", i, 0), ("rs", i, 1)],
            [("res", j, 0), ("res", j, 1)], bias=rs[i][:, 1:2], scale=rs[i][:, 0:1])
        for hf in range(2):
            bi = 5 + hf
            for cc in range(4):
                c = hf * 4 + cc
                tr(ps[bi][:, cc * 128:(cc + 1) * 128], res[:, j, c * 128:(c + 1) * 128], ident32[:, :],
                   [("res", j, hf), "const"], [("ps", bi)])
                act(hT[:, c, j * 128:(j + 1) * 128], ps[bi][:, cc * 128:(cc + 1) * 128], AF.Identity,
                    [("ps", bi), "const"], [("hT", j, c)], bias=col("b0", c), scale=col("g0", c))
        if not light:
            for hf in range(2):
                sl = slice(hf * 512, (hf + 1) * 512)
                tt("pool", res[:, j, sl], res[:, j, sl], rows[:, 0, sl], ALU.mult, [("res", j, hf), "const"], [("res", j, hf)])
                tt("pool", res[:, j, sl], res[:, j, sl], rows[:, 1, sl], ALU.add, [("res", j, hf), "const"], [("res", j, hf)])
    ln0_block.cnt = 0

    hTkeys_k = lambda k: [("hT", j, k) for j in range(NB)]

    def proj_fm(wv, cc, ncolw, bi, act_keys_k, src):
        for k in range(8):
            mm(ps[bi][:, :], wv[0][:, k, cc * 128:(cc + 1) * 128], src[:, k, :], k == 0, k == 7,
               [("w", wv[1])] + act_keys_k(k), [("ps", bi)])

    def gelu_evac(bi, bcol, out_ap, outkey):
        a32, q32 = t32[0], t32[1]
        act(a32[:, :], ps[bi][:, :], AF.Identity, [("ps", bi), "const"], [K[0]], bias=bcol)
        act(q32[:, :], ps[bi][:, :], AF.Square, [("ps", bi), "const"], [K[1]], bias=bcol)
        ts("dve", q32[:, :], q32[:, :], 0.044715, 1.0, ALU.mult, ALU.add, [K[1]], [K[1]])
        tt("dve", q32[:, :], q32[:, :], a32[:, :], ALU.mult, [K[1], K[0]], [K[1]])
        act(q32[:, :], q32[:, :], AF.Sigmoid, [K[1]], [K[1]], scale=1.5957691216057308)
        tt("dve", out_ap, a32[:, :], q32[:, :], ALU.mult, [K[1], K[0]], [outkey])

    hc = [0]

    def head_core(hd, wf, hh, light, wq=None, wgg=None):
        par = hc[0] % 2
        hc[0] += 1
        sf, k32, lf, b32, e32, q32 = t32[0], t32[1], t32[2], t32[3], t32[4], t32[5]
        bf_ = workbank()
        proj_fm(wf, hh, 512, bf_, hTkeys_k, hT)
        act(sf[:, :], ps[bf_][:, :], AF.Sigmoid, [("ps", bf_), "const"], [K[0]], bias=col("bin", 24 + hd))
        ts("dve", k32[:, :], sf[:, :], col("noml", hd), col("oml", hd), ALU.mult, ALU.add, [K[0], "noml", "oml"], [K[1]])
        act(lf[:, :], sf[:, :], AF.Ln, [K[0], "oml", "lb"], [K[2]], bias=col("lb", hd), scale=col("oml", hd))
        P.add("dve", lambda e: e.tensor_tensor_scan(out=b32[:, :], data0=reset[:, :], data1=lf[:, :], initial=0.0,
                                                    op0=ALU.mult, op1=ALU.add), [K[2], "const"], [K[3]])
        act(dec[par][:, :], b32[:, :].rearrange("p (c k) -> p c k", k=32)[:, :, 31], AF.Exp, [K[3]], [("dec", par)])
        act(e32[:, :], b32[:, :], AF.Exp, [K[3]], [K[4]], scale=-1.0)
        tt("pool", kd[par][:, :], k32[:, :], e32[:, :], ALU.mult, [K[1], K[4]], [("kd", par)])
        tt("pool", ke[par][:, :].rearrange("p (c k) -> p c k", k=32), kd[par][:, :].rearrange("p (c k) -> p c k", k=32),
           dec[par][:, :].unsqueeze(2).to_broadcast([128, 16, 32]), ALU.mult, [("kd", par), ("dec", par)], [("ke", par)])
        if not light:
            bq = workbank()
            proj_fm(wq, hh, 512, bq, hTkeys_k, hT)
            act(q32[:, :], ps[bq][:, :], AF.Sigmoid, [("ps", bq), "const"], [K[5]], bias=col("bin", 16 + hd))
            stt("dve", q32[:, :], ps[bq][:, :], col("bin", 16 + hd), q32[:, :], ALU.add, ALU.mult,
                [("ps", bq), K[5], "const"], [K[5]])
            act(e32[:, :], b32[:, :], AF.Exp, [K[3], ("kd", par)], [K[4]])
            tt("pool", qd[par][:, :], q32[:, :], e32[:, :], ALU.mult, [K[5], K[4]], [("qd", par)])
            bg = workbank()
            proj_fm(wgg, hh, 512, bg, hTkeys_k, hT)
            act(lf[:, :], ps[bg][:, :], AF.Sigmoid, [("ps", bg), "const"], [K[2]], bias=col("bin", 40 + hd))
            stt("dve", gsb[par][:, :], ps[bg][:, :], col("bin", 40 + hd), lf[:, :], ALU.add, ALU.mult,
                [("ps", bg), K[2], "const"], [("gs", par)])
        bt = workbank()
        pb = ps[bt][:, :].bitcast(BF16)
        for j in range(NB):
            tr(pb[:, j * 128:(j + 1) * 128], ke[par][:, j * 128:(j + 1) * 128], identb[:, :],
               [("ke", par), "identb"], [("ps", bt)])
        cp("act", ketm[par][:, :, :].rearrange("p a b -> p (a b)"), pb[:, 0:512], [("ps", bt)], [("ketm", par)])
        qc = [0]

        def quarter():
            i = qc[0] % 3
            qc[0] += 1
            return 5 + i, 0
        for j in range(NB):
            js = slice(j * 128, (j + 1) * 128)
            vkey = ("v", j, hd // 4)
            if not light:
                qb, qi = quarter()
                scp = ps[qb][:, qi * 128:(qi + 1) * 128]
                mm(scp, kd[par][:, js], qd[par][:, js], True, True, [("kd", par), ("qd", par)], [("ps", qb)])
                si = (j) % 4
                tt("dve", scb[si][:, :], scp, cmask[:, :], ALU.mult, [("ps", qb), "const"], [("scb", si)])
                mm(ps[4][:, js], vtm[:, j, hd * 128:(hd + 1) * 128], scb[si][:, :], True, False,
                   [vkey, ("scb", si)], [("ps", 4)], skip_group_check=True)
            for c in range(4):
                tsl = slice(j * 128 + c * 32, j * 128 + (c + 1) * 32)
                if not light:
                    mm(ps[4][:, tsl], Sbf[:, hd, :], qd[par][:, tsl], False, True,
                       [("Sbf", hd), ("qd", par)], [("ps", 4)], skip_group_check=True)
                qb, qi = quarter()
                kvp = ps[qb][:, qi * 128:(qi + 1) * 128]
                kw = {"tile_position": (96, 0)} if c == 3 else {}
                mm(kvp, ketm[par][c * 32:(c + 1) * 32, j, :], vtm[c * 32:(c + 1) * 32, j, hd * 128:(hd + 1) * 128],
                   True, True, [("ketm", par), vkey], [("ps", qb)], **kw)
                stt("dve", S32[:, hd, :], S32[:, hd, :], dec[par][:, j * 4 + c:j * 4 + c + 1], kvp, ALU.mult, ALU.add,
                    [("S32", hd), ("dec", par), ("ps", qb)], [("S32", hd)])
                if not light:
                    cp("act", Sbf[:, hd, :], S32[:, hd, :], [("S32", hd)], [("Sbf", hd)])
        if light:
            return
        act(osq[:, :], ps[4][:, :], AF.Square, [("ps", 4)], ["osq"])
        bss = workbank()
        mm(ps[bss][:, :], onesb[:, :], osq[:, :], True, True, ["osq", "onesb"], [("ps", bss)])
        act(e32[:, :], ps[bss][:, :], AF.Ln, [("ps", bss), ("qd", par)], [K[4]], bias=RMS_EPS, scale=1.0 / 128.0)
        act(e32[:, :], e32[:, :], AF.Exp, [K[4]], [K[4]], scale=-0.5)
        stt("dve", q32[:, :], ps[4][:, :], col("og", hd), e32[:, :], ALU.mult, ALU.mult,
            [("ps", 4), K[4], ("qd", par), "const"], [K[5]])
        tt("pool", Abuf[:, 8 + hd, :], q32[:, :], gsb[par][:, :], ALU.mult, [K[5], ("gs", par)], [("A", 8 + hd)])

    def v_proj(wv, half):
        for j in range(NB):
            bi = workbank()
            for k in range(8):
                mm(ps[bi][:, :], hT[:, k, j * 128:(j + 1) * 128], wv[0][:, k, :], k == 0, False,
                   [("w", wv[1]), ("hT", j, k)], [("ps", bi)])
            mm(ps[bi][:, :], onesb[0:1, :], browb[0:1, half * 512:(half + 1) * 512], False, True,
               ["onesb", "browb"], [("ps", bi)])
            cp("act", vtm[:, j, half * 512:(half + 1) * 512], ps[bi][:, :], [("ps", bi)], [("v", j, half)])

    def light_tile(t0):
        for j in range(NB):
            ln0_block(xl, t0, j, True)
        for half in range(2):
            s, wv = WS.take("win", 8 + half)
            v_proj((wv, s), half)
            WS.release(s)
        for hg in range(2):
            s, wv = WS.take("win", 6 + hg)
            for hh in range(4):
                head_core(hg * 4 + hh, (wv, s), hh, True)
            WS.release(s)

    def full_tile(t0):
        for j in range(NB):
            ln0_block(xf, t0, j, False)
        for sl in range(4):
            s, wv = WS.take("win", sl)
            for cc in range(4):
                c = (sl % 2) * 4 + cc
                bi = workbank()
                proj_fm((wv, s), cc, 512, bi, hTkeys_k, hT)
                if sl < 2:
                    gelu_evac(bi, col("bin", c), Abuf[:, c, :], ("A", c))
                else:
                    gelu_evac(bi, col("bin", 8 + c), Abuf[:, 8 + c, :], ("A", 8 + c))
            WS.release(s)
        for j in range(NB):
            pb = ps[4][:, :].bitcast(BF16)
            for c in range(8):
                tr(pb[:, c * 128:(c + 1) * 128], Abuf[:, 8 + c, j * 128:(j + 1) * 128], identb[:, :],
                   [("A", 8 + c), "identb"], [("ps", 4)])
            i = ln_stats(pb[:, 0:512], pb[:, 512:1024], [("ps", 4)])
            act(vn[:, j, :], pb[:, :], AF.Identity, [("ps", 4), ("rs", i, 0), ("rs", i, 1)], [("vn", j)],
                bias=rs[i][:, 1:2], scale=rs[i][:, 0:1])
        for g in range(8):
            bi = workbank()
            for j in range(NB):
                mm(ps[bi][:, j * 128:(j + 1) * 128], vn[:, j, g * 128:(g + 1) * 128], wstb[:, g, :], True, True,
                   [("vn", j), "wstb"], [("ps", bi)])
            stt("dve", t32[0][:, :].rearrange("p (a b) -> p a b", a=4), ps[bi][:, :].rearrange("p (a b) -> p a b", a=4),
                col("gmg", g), Cg[:, g:g + 1, :].to_broadcast([128, 4, 128]), ALU.mult, ALU.add,
                [("ps", bi), ("Cg", g), "const"], [K[0]])
            tt("dve", Abuf[:, g, :], t32[0][:, :], Abuf[:, g, :], ALU.mult, [K[0], ("A", g)], [("A", g)])
        for half in range(2):
            s, wv = WS.take("win", 8 + half)
            v_proj((wv, s), half)
            WS.release(s)
        for hg in range(2):
            sq, wq = WS.take("win", 4 + hg)
            sf_, wf = WS.take("win", 6 + hg)
            sg, wgg = WS.take("win", 10 + hg)
            for hh in range(4):
                head_core(hg * 4 + hh, (wf, sf_), hh, False, (wq, sq), (wgg, sg))
            WS.release(sq)
            WS.release(sf_)
            WS.release(sg)
        yakeys = lambda k: [("A", k)]
        ybkeys = lambda k: [("A", 8 + k)]
        for sl in range(2):
            for part in range(2):
                sw, ww = WS.take("wa" if part == 0 else "wb", sl)
                sgt, wgt = WS.take("win", (12 if part == 0 else 14) + sl)
                for cc in range(4):
                    m = sl * 4 + cc
                    b1 = workbank()
                    b2 = workbank()
                    src = Abuf[:, 0:8, :] if part == 0 else Abuf[:, 8:16, :]
                    proj_fm((ww, sw), cc, 512, b1, yakeys if part == 0 else ybkeys, src)
                    proj_fm((wgt, sgt), cc, 512, b2, hTkeys_k, hT)
                    act(t32[0][:, :], ps[b2][:, :], AF.Sigmoid, [("ps", b2), "const"], [K[0]],
                        bias=col("bin", (48 if part == 0 else 56) + m))
                    if part == 0:
                        tt("dve", Abuf[:, 16 + m, :], ps[b1][:, :], t32[0][:, :], ALU.mult, [("ps", b1), K[0]], [("A", 16 + m)])
                    else:
                        tt("dve", t32[1][:, :], ps[b1][:, :], t32[0][:, :], ALU.mult, [("ps", b1), K[0]], [K[1]])
                        tt("pool", Abuf[:, 16 + m, :], Abuf[:, 16 + m, :], t32[1][:, :], ALU.add, [K[1], ("A", 16 + m)], [("A", 16 + m)])
                WS.release(sw)
                WS.release(sgt)
        for sl in range(2):
            s, wv = WS.take("wo", sl)
            for j in range(NB):
                bi = workbank()
                for k in range(8):
                    mm(ps[bi][:, :], Abuf[:, 16 + k, j * 128:(j + 1) * 128], wv[:, k, :], k == 0, k == 7,
                       [("w", s), ("A", 16 + k)], [("ps", bi)])
                hs = slice(sl * 512, (sl + 1) * 512)
                stt("dve", res[:, j, hs], res[:, j, hs], ALPHA, ps[bi][:, :], ALU.mult, ALU.add,
                    [("res", j, sl), ("ps", bi)], [("res", j, sl)])
            WS.release(s)
        for j in range(NB):
            i = ln_stats(res[:, j, 0:512], res[:, j, 512:1024], [("res", j, 0), ("res", j, 1)])
            act(res[:, j, :], res[:, j, :], AF.Identity, [("res", j, 0), ("res", j, 1), ("rs", i, 0), ("rs", i, 1)],
                [("res", j, 0), ("res", j, 1)], bias=rs[i][:, 1:2], scale=rs[i][:, 0:1])
            for hf in range(2):
                bi = 5 + hf
                for cc in range(4):
                    c = hf * 4 + cc
                    tr(ps[bi][:, cc * 128:(cc + 1) * 128], res[:, j, c * 128:(c + 1) * 128], ident32[:, :],
                       [("res", j, hf), "const"], [("ps", bi)])
                    act(hT[:, c, j * 128:(j + 1) * 128], ps[bi][:, cc * 128:(cc + 1) * 128], AF.Identity,
                        [("ps", bi), "const"], [("hT", j, c)], bias=col("b1", c), scale=col("g1", c))
            for hf in range(2):
                sl_ = slice(hf * 512, (hf + 1) * 512)
                tt("pool", res[:, j, sl_], res[:, j, sl_], rows[:, 2, sl_], ALU.mult, [("res", j, hf), "const"], [("res", j, hf)])
                tt("pool", res[:, j, sl_], res[:, j, sl_], rows[:, 3, sl_], ALU.add, [("res", j, hf), "const"], [("res", j, hf)])
        for i in range(11):
            sg_, wg_ = WS.take("wg", i)
            su_, wu_ = WS.take("wu", i)
            for cc in range(2):
                fc = 2 * i + cc
                b1 = workbank()
                b2 = workbank()
                proj_fm((wg_, sg_), cc, 256, b1, hTkeys_k, hT)
                proj_fm((wu_, su_), cc, 256, b2, hTkeys_k, hT)
                act(t32[0][:, :], ps[b1][:, :], AF.Sigmoid, [("ps", b1)], [K[0]])
                tt("dve", t32[0][:, :], ps[b1][:, :], t32[0][:, :], ALU.mult, [("ps", b1), K[0]], [K[0]])
                tt("dve", Abuf[:, fc, :], ps[b2][:, :], t32[0][:, :], ALU.mult, [("ps", b2), K[0]], [("A", fc)])
            WS.release(sg_)
            WS.release(su_)
        for hf in range(2):
            k0 = 0
            for q in range(4):
                s, wv = WS.take("wd", hf * 4 + q)
                nk = WD_SPLIT[q]
                for j in range(NB):
                    for kk in range(nk):
                        k = k0 + kk
                        mm(ps[4 + j][:, :], Abuf[:, k, j * 128:(j + 1) * 128], wv[:, kk, :], k == 0, k == NFC - 1,
                           [("w", s), ("A", k)], pk(4 + j))
                k0 += nk
                WS.release(s)
            hs = slice(hf * 512, (hf + 1) * 512)
            for j in range(NB):
                stt("dve", res[:, j, hs], res[:, j, hs], ALPHA, ps[4 + j][:, :], ALU.mult, ALU.add,
                    [("res", j, hf)] + pk(4 + j), [("res", j, hf)])
        for j in range(NB):
            s_ = j % 2
            dma("sp", pin[s_][:, :], pf[t0 + j * 128:t0 + (j + 1) * 128, :], [], [("pin", s_)], ("p", s_))
            for kc in range(2):
                tr(ps[5][:, kc * 128:(kc + 1) * 128], pin[s_][:, kc * 128:(kc + 1) * 128], ident32[:, :],
                   [("pin", s_), "const"], [("ps", 5)])
                cp("act", pT[:, kc, j * 128:(j + 1) * 128], ps[5][:, kc * 128:(kc + 1) * 128], [("ps", 5)], [("pT", j)])
        spl, wpl = WS.take("wple", 0)
        for sl in range(2):
            s, wv = WS.take("wpg", sl)
            hs = slice(sl * 512, (sl + 1) * 512)
            for j in range(NB):
                b1 = workbank()
                b2 = workbank()
                for k in range(8):
                    mm(ps[b1][:, :], hT[:, k, j * 128:(j + 1) * 128], wv[:, k, :], k == 0, False,
                       [("w", s), ("hT", j, k)], [("ps", b1)])
                mm(ps[b1][:, :], onesb[0:1, :], browb[0:1, D + sl * 512:D + (sl + 1) * 512], False, True,
                   ["onesb", "browb"], [("ps", b1)])
                for k in range(2):
                    mm(ps[b2][:, :], pT[:, k, j * 128:(j + 1) * 128], wpl[:, k, hs], k == 0, k == 1,
                       [("w", spl), ("pT", j)], [("ps", b2)])
                act(t32[0][:, :], ps[b1][:, :], AF.Sigmoid, [("ps", b1)], [K[0]])
                tt("dve", t32[0][:, :], ps[b2][:, :], t32[0][:, :], ALU.mult, [("ps", b2), K[0]], [K[0]])
                tt("pool", res[:, j, hs], res[:, j, hs], t32[0][:, :], ALU.add, [K[0], ("res", j, sl)], [("res", j, sl)])
            WS.release(s)
        WS.release(spl)
        for j in range(NB):
            i = ln_stats(res[:, j, 0:512], res[:, j, 512:1024], [("res", j, 0), ("res", j, 1)])
            act(res[:, j, :], res[:, j, :], AF.Identity, [("res", j, 0), ("res", j, 1), ("rs", i, 0), ("rs", i, 1)],
                [("res", j, 0), ("res", j, 1)], bias=rs[i][:, 1:2], scale=rs[i][:, 0:1])
            for hf in range(2):
                sl_ = slice(hf * 512, (hf + 1) * 512)
                tt("pool", res[:, j, sl_], res[:, j, sl_], rows[:, 4, sl_], ALU.mult, [("res", j, hf), "const"], [("res", j, hf)])
                tt("pool", res[:, j, sl_], res[:, j, sl_], rows[:, 5, sl_], ALU.add, [("res", j, hf), "const"], [("res", j, hf)])
            dma("sp", yo[t0 + j * 128:t0 + (j + 1) * 128, :], res[:, j, :], [("res", j, 0), ("res", j, 1)], [("yout", j)], ("out", j))

    setup()
    ngrp = len(cast_groups)
    gi = 1
    for lt in range(NLT if stage >= 1 else 0):
        if gi < ngrp:
            emit_cast(gi)
            gi += 1
        light_tile(lt * T)
    while gi < ngrp:
        emit_cast(gi)
        gi += 1
    for hd in range(NH):
        ts("dve", S32[:, hd, :], S32[:, hd, :], col("flag", 0), 0.0, ALU.mult, ALU.add, [("S32", hd), "const"], [("S32", hd)])
        cp("act", Sbf[:, hd, :], S32[:, hd, :], [("S32", hd)], [("Sbf", hd)])
    for ft in range(NFT if stage >= 2 else 0):
        full_tile(ft * T)
    if stage < 2:
        for j in range(NB):
            P.add("pool", lambda e, j=j: e.memset(res[:, j, :], 1.0), [("res", j, 0), ("res", j, 1)], [("res", j, 0), ("res", j, 1)])
            dma("sp", yo[j * 128:(j + 1) * 128, :], res[:, j, :], [("res", j, 0), ("res", j, 1)], [("yout", j)], ("out", j))
    P.add("sp", None, [("yout", j) for j in range(NB)], [])

    P.finalize()
    semkeys = list(P.semops.keys())
    sems = {k: nc.alloc_semaphore("s_" + "_".join(str(x) for x in (k if isinstance(k, tuple) else (k,)))) for k in semkeys}
    with nc.Block() as block:
        @block.sync
        def _(e):
            P.emit("sp", e, sems)

        @block.tensor
        def _(e):
            P.emit("pe", e, sems)

        @block.scalar
        def _(e):
            P.emit("act", e, sems)

        @block.vector
        def _(e):
            P.emit("dve", e, sems)

        @block.gpsimd
        def _(e):
            P.emit("pool", e, sems)
    return nc


def _blocks(W, ncol):
    K, N = W.shape
    kc = K // 128
    nb = N // ncol
    return np.ascontiguousarray(W.reshape(kc, 128, nb, ncol).transpose(2, 1, 0, 3).reshape(nb, 128, kc * ncol))


def _colv(v):
    v = np.asarray(v, np.float32).reshape(-1)
    return v.reshape(-1, 128).T


def prep_shared(inp):
    f = lambda k: np.asarray(inp[k], np.float32)
    sh = {}
    sh["win"] = _blocks(f("w_in")[0], 512)
    sh["wa"] = _blocks(f("w_a")[0], 512)
    sh["wb"] = _blocks(f("w_b")[0], 512)
    sh["wo"] = _blocks(f("w_o")[0], 512)
    sh["wg"] = _blocks(f("w_ffn_gate")[0], 256)
    sh["wu"] = _blocks(f("w_ffn_up")[0], 256)
    sh["wpg"] = _blocks(f("w_ple_gate")[0], 512)
    sh["wple"] = _blocks(f("w_ple")[0], 1024)
    wd = f("w_ffn_down")[0]
    wdb = np.zeros((8, 128, 6, 512), np.float32)
    for hf in range(2):
        k0 = 0
        for q, nk in enumerate(WD_SPLIT):
            blk = wd[k0 * 128:(k0 + nk) * 128, hf * 512:(hf + 1) * 512].reshape(nk, 128, 512).transpose(1, 0, 2)
            wdb[hf * 4 + q, :, :nk, :] = blk
            k0 += nk
    sh["wd"] = wdb.reshape(8, 128, 6 * 512)
    cols = np.zeros((128, NCOL), np.float32)

    def put(name, arr):
        c0, w = COLS[name]
        cols[:, c0:c0 + w] = arr
    put("g0", _colv(f("ln0_g")))
    put("b0", _colv(f("ln0_b")))
    put("bin", _colv(f("b_in")[0]))
    put("gmg", _colv(f("gm_norm_g")[0]))
    put("gmb", _colv(f("gm_norm_b")[0]))
    put("l0", _colv(f("hg_lb_logits")[0]))
    put("l1", _colv(f("hg_lb_logits")[1]))
    put("og", _colv(f("hg_norm_g")[0]))
    put("g1", _colv(f("ln1_g")[0]))
    put("b1", _colv(f("ln1_b")[0]))
    sh["cols"] = cols
    rows = [f("ln0_g"), f("ln0_b"), f("ln1_g")[0], f("ln1_b")[0], f("ln2_g")[0], f("ln2_b")[0]]
    sh["rows"] = np.ascontiguousarray(np.broadcast_to(np.concatenate(rows)[None, :], (128, 6 * D)))
    ws = f("gm_w_s")[0]
    sh["wst"] = np.ascontiguousarray(ws.transpose(2, 0, 1).reshape(128, 8 * 128))
    sh["bsb"] = np.ascontiguousarray(np.broadcast_to(f("gm_b_s")[0].reshape(1, 8 * 128), (128, 8 * 128)))
    s_ = np.arange(128)[:, None]
    t_ = np.arange(128)[None, :]
    sh["cmask"] = ((s_ // 32 == t_ // 32) & (s_ <= t_)).astype(np.float32)
    r = np.ones((128, T), np.float32)
    r[:, ::32] = 0.0
    sh["reset"] = r
    sh["ident"] = np.eye(128, dtype=np.float32)
    sh["brow"] = np.concatenate([f("b_in")[0][4096:5120], f("b_ple_gate")[0]])[None, :].astype(np.float32)
    return sh


_NC_CACHE = {}


def run(inp, NLT, NFT, seq_half, stage=2, ncores=8):
    x = np.asarray(inp["x"], np.float32)
    p = np.asarray(inp["p"], np.float32)[0]
    sh = prep_shared(inp)
    key = (NLT, NFT, stage)
    if key not in _NC_CACHE:
        _NC_CACHE[key] = build_program(NLT, NFT, stage)
    nc = _NC_CACHE[key]
    in_maps = []
    B = x.shape[0]
    for c in range(8):
        b, hf = c // 2, c % 2
        m = dict(sh)
        cols = sh["cols"].copy()
        cols[:, COLS["flag"][0]] = float(hf)
        m["cols"] = cols
        m["xl"] = np.ascontiguousarray(x[b, 0:max(NLT, 1) * T])
        m["xf"] = np.ascontiguousarray(x[b, hf * seq_half:(hf + 1) * seq_half])
        m["pf"] = np.ascontiguousarray(p[b, hf * seq_half:(hf + 1) * seq_half])
        in_maps.append(m)
    r = run_bass_kernel_spmd(nc, in_maps[:ncores], core_ids=list(range(ncores)))
    out = np.zeros((B, 2 * seq_half, D), np.float32)
    for c in range(ncores):
        b, hf = c // 2, c % 2
        out[b, hf * seq_half:(hf + 1) * seq_half] = r.results[c]["y"]
    return out


def kernel(**inputs):
    return run(inputs, NTOK // T, NTOK // T, NTOK)
```
